# Optimizing a Trainium2 kernel written in Bass

```python
import jax, jax.numpy as jnp
from jax import lax
import numpy as np

D_MODEL = 2048
BATCH = 2
SEQ = 8192
DEPTH = 2

RET_HEADS = 4
RET_HEAD_DIM = D_MODEL // 2 // RET_HEADS
RET_WIDTH = RET_HEADS * RET_HEAD_DIM
CONV_WIDTH = D_MODEL - RET_WIDTH
CONV_K = 3
RET_CHUNK = 128
EVEN_IN = 4 * RET_WIDTH + 4 * CONV_WIDTH
ATTN_HEAD_DIM = 64
ATTN_Q_HEADS = D_MODEL // ATTN_HEAD_DIM
ATTN_KV_HEADS = ATTN_Q_HEADS // 8
ATTN_WIDTH = ATTN_Q_HEADS * ATTN_HEAD_DIM
KV_WIDTH = ATTN_KV_HEADS * ATTN_HEAD_DIM
ODD_IN = 2 * ATTN_WIDTH + 2 * KV_WIDTH
WINDOW = 128
BLOCK = 128
ROPE_THETA = 10000.0
EPS = 1e-6
N_EVEN = (DEPTH + 1) // 2
N_ODD = DEPTH // 2

kernel_name = "hybrid_retention_shortconv_swa_sinks"


def rms_norm(x, w):
    xf = x.astype(jnp.float32)
    y = xf * lax.rsqrt(jnp.mean(xf * xf, axis=-1, keepdims=True) + EPS)
    return (y * w.astype(jnp.float32)).astype(x.dtype)


def rms_norm_nogain(x):
    xf = x.astype(jnp.float32)
    return (xf * lax.rsqrt(jnp.mean(xf * xf, axis=-1, keepdims=True) + EPS)).astype(x.dtype)


def rope(x, pos):
    d = x.shape[-1]
    inv = 1.0 / (ROPE_THETA ** (jnp.arange(0, d, 2, dtype=jnp.float32) / d))
    ang = pos.astype(jnp.float32)[:, None] * inv[None, :]
    cos = jnp.cos(ang)[None, :, None, :]
    sin = jnp.sin(ang)[None, :, None, :]
    xf = x.astype(jnp.float32)
    x1, x2 = xf[..., : d // 2], xf[..., d // 2:]
    return jnp.concatenate([x1 * cos - x2 * sin, x2 * cos + x1 * sin], axis=-1).astype(x.dtype)


def retention_chunkwise(q, k, v):
    b, s, h, d = q.shape
    c = RET_CHUNK
    n = s // c
    dt = q.dtype
    log_g = jnp.log(1.0 - 2.0 ** (-5.0 - jnp.arange(h, dtype=jnp.float32)))
    idx = jnp.arange(c, dtype=jnp.float32)
    diff = idx[:, None] - idx[None, :]
    intra = jnp.where(diff >= 0, jnp.exp(log_g[:, None, None] * jnp.maximum(diff, 0.0)), 0.0).astype(dt)
    q_dec = jnp.exp(log_g[:, None] * (idx[None, :] + 1.0)).astype(dt)
    k_dec = jnp.exp(log_g[:, None] * (c - 1.0 - idx[None, :])).astype(dt)
    chunk_dec = jnp.exp(log_g * c).astype(dt)
    qc = q.reshape(b, n, c, h, d)
    kc = (k * (d ** -0.5)).reshape(b, n, c, h, d)
    vc = v.reshape(b, n, c, h, d)
    scores = jnp.einsum('bnihd,bnjhd->bnhij', qc, kc) * intra[None, None]
    inner = jnp.einsum('bnhij,bnjhe->bnihe', scores, vc)
    kv = jnp.einsum('bnjhd,bnjhe,hj->bnhde', kc, vc, k_dec)

    def step(state, kv_n):
        return chunk_dec[None, :, None, None] * state + kv_n, state

    _, prev = lax.scan(step, jnp.zeros((b, h, d, d), dt), jnp.moveaxis(kv, 1, 0))
    prev = jnp.moveaxis(prev, 0, 1)
    cross = jnp.einsum('bnihd,bnhde,hi->bnihe', qc, prev, q_dec)
    return (inner + cross).reshape(b, s, h, d)


def even_mixer(h, w_in, conv_w, w_out, pos):
    b, s, _ = h.shape
    proj = h @ w_in
    q, k, v, g_ret, gate_b, gate_c, u, g_conv = jnp.split(proj, 8, axis=-1)
    shp = (b, s, RET_HEADS, RET_HEAD_DIM)
    q = rope(q.reshape(shp), pos)
    k = rope(k.reshape(shp), pos)
    o = retention_chunkwise(q, k, v.reshape(shp))
    ret_out = rms_norm_nogain(o).reshape(b, s, RET_WIDTH) * jax.nn.silu(g_ret)
    conv = lax.conv_general_dilated(
        gate_c * u, conv_w[:, None, :].astype(u.dtype), window_strides=(1,),
        padding=[(CONV_K - 1, 0)], dimension_numbers=('NWC', 'WIO', 'NWC'),
        feature_group_count=CONV_WIDTH)
    conv_out = gate_b * conv * jax.nn.silu(g_conv)
    return jnp.concatenate([ret_out, conv_out], axis=-1) @ w_out


def swa_sinks(q, k, v, sinks):
    b, s, hq, d = q.shape
    hk = k.shape[2]
    g = hq // hk
    n = s // BLOCK
    qb = q.reshape(b, n, BLOCK, hk, g, d)
    kb = k.reshape(b, n, BLOCK, hk, d)
    vb = v.reshape(b, n, BLOCK, hk, d)
    pad = ((0, 0), (1, 0), (0, 0), (0, 0), (0, 0))
    kk = jnp.concatenate([jnp.pad(kb, pad)[:, :-1], kb], axis=2)
    vv = jnp.concatenate([jnp.pad(vb, pad)[:, :-1], vb], axis=2)
    scores = jnp.einsum('bnihgd,bnjhd->bnhgij', qb, kk).astype(jnp.float32) * (d ** -0.5)
    qi = jnp.arange(BLOCK)[:, None] + BLOCK
    kj = jnp.arange(2 * BLOCK)[None, :]
    band = (kj <= qi) & (qi - kj < WINDOW)
    valid = band[None] & ((jnp.arange(n)[:, None, None] > 0) | (kj >= BLOCK)[None])
    scores = jnp.where(valid[None, :, None, None], scores, -1e30)
    sink = sinks.astype(jnp.float32).reshape(hk, g)[None, None, :, :, None, None]
    m = jnp.maximum(jnp.max(scores, axis=-1, keepdims=True), sink)
    p = jnp.exp(scores - m)
    p = p / (jnp.sum(p, axis=-1, keepdims=True) + jnp.exp(sink - m))
    o = jnp.einsum('bnhgij,bnjhd->bnihgd', p.astype(v.dtype), vv)
    return o.reshape(b, s, hq, d)


def odd_mixer(h, w_in, q_norm_w, k_norm_w, sinks, w_out, pos):
    b, s, _ = h.shape
    proj = h @ w_in
    q, k, v, gate = jnp.split(proj, [ATTN_WIDTH, ATTN_WIDTH + KV_WIDTH, ATTN_WIDTH + 2 * KV_WIDTH], axis=-1)
    q = rope(rms_norm(q.reshape(b, s, ATTN_Q_HEADS, ATTN_HEAD_DIM), q_norm_w), pos)
    k = rope(rms_norm(k.reshape(b, s, ATTN_KV_HEADS, ATTN_HEAD_DIM), k_norm_w), pos)
    v = v.reshape(b, s, ATTN_KV_HEADS, ATTN_HEAD_DIM)
    o = swa_sinks(q, k, v, sinks).reshape(b, s, ATTN_WIDTH)
    return (o * jax.nn.silu(gate)) @ w_out


def setup_inputs(seed: int = 0) -> dict:
    key = jax.random.key(seed)
    ks = jax.random.split(key, 11)
    f32 = jnp.float32
    x = jax.random.normal(ks[0], (BATCH, SEQ, D_MODEL), f32)
    ev_norm_w = 1.0 + 0.02 * jax.random.normal(ks[1], (N_EVEN, D_MODEL), f32)
    ev_w_in = jax.random.normal(ks[2], (N_EVEN, D_MODEL, EVEN_IN), f32) * D_MODEL ** -0.5
    ev_conv_w = jax.random.normal(ks[3], (N_EVEN, CONV_K, CONV_WIDTH), f32) * CONV_K ** -0.5
    ev_w_out = jax.random.normal(ks[4], (N_EVEN, D_MODEL, D_MODEL), f32) * D_MODEL ** -0.5
    od_norm_w = 1.0 + 0.02 * jax.random.normal(ks[5], (N_ODD, D_MODEL), f32)
    od_w_in = jax.random.normal(ks[6], (N_ODD, D_MODEL, ODD_IN), f32) * D_MODEL ** -0.5
    od_q_norm_w = 1.0 + 0.02 * jax.random.normal(ks[7], (N_ODD, ATTN_HEAD_DIM), f32)
    od_k_norm_w = 1.0 + 0.02 * jax.random.normal(ks[8], (N_ODD, ATTN_HEAD_DIM), f32)
    od_sinks = 0.5 * jax.random.normal(ks[9], (N_ODD, ATTN_Q_HEADS), f32)
    od_w_out = jax.random.normal(ks[10], (N_ODD, ATTN_WIDTH, D_MODEL), f32) * ATTN_WIDTH ** -0.5
    return {"x": x, "ev_norm_w": ev_norm_w, "ev_w_in": ev_w_in, "ev_conv_w": ev_conv_w,
            "ev_w_out": ev_w_out, "od_norm_w": od_norm_w, "od_w_in": od_w_in,
            "od_q_norm_w": od_q_norm_w, "od_k_norm_w": od_k_norm_w, "od_sinks": od_sinks,
            "od_w_out": od_w_out}


def reference(x, ev_norm_w, ev_w_in, ev_conv_w, ev_w_out, od_norm_w, od_w_in,
              od_q_norm_w, od_k_norm_w, od_sinks, od_w_out):
    pos = jnp.arange(x.shape[1])
    for layer in range(DEPTH):
        i = layer // 2
        if layer % 2 == 0:
            h = rms_norm(x, ev_norm_w[i])
            x = x + even_mixer(h, ev_w_in[i], ev_conv_w[i], ev_w_out[i], pos)
        else:
            h = rms_norm(x, od_norm_w[i])
            x = x + odd_mixer(h, od_w_in[i], od_q_norm_w[i], od_k_norm_w[i], od_sinks[i], od_w_out[i], pos)
    return x
```

```python
import numpy as np
from contextlib import ExitStack
import concourse.bass as bass
import concourse.mybir as mybir
from concourse.bass_utils import run_bass_kernel_spmd

F32 = mybir.dt.float32
BF16 = mybir.dt.bfloat16
AF = mybir.ActivationFunctionType
ALU = mybir.AluOpType
AX = mybir.AxisListType

D = 2048
EPS = 1e-6
THETA = 10000.0
SAME_ENGINE_SYNC = True
DEBUG = False


class Buf:
    __slots__ = ("name", "lw", "rd")

    def __init__(self, name):
        self.name = name
        self.lw = None
        self.rd = []


class Op:
    __slots__ = ("eng", "fn", "reads", "writes", "dma", "key", "eidx", "deps", "sig",
                 "signo", "cum", "waits", "gidx", "xw")


class Prog:
    ENGS = ("pe", "act", "dve", "pool", "sp")

    def __init__(self):
        self.ops = []
        self.bufs = {}
        self.keycnt = {}
        self.pending = {e: [] for e in self.ENGS}
        self.pending_keys = {e: {} for e in self.ENGS}
        self.nbar = 0

    def buf(self, name):
        b = self.bufs.get(name)
        if b is None:
            b = self.bufs[name] = Buf(name)
        return b

    def _bl(self, xs):
        out = []
        for x in xs:
            if x is None:
                continue
            if isinstance(x, (list, tuple)):
                out.extend(self._bl(x))
            else:
                out.append(self.buf(x) if isinstance(x, str) else x)
        return out

    def add(self, eng, fn, reads=(), writes=(), key=None):
        op = Op()
        op.eng = eng
        op.fn = fn
        op.reads = self._bl(reads)
        op.writes = self._bl(writes)
        if self.pending[eng]:
            op.reads = op.reads + self.pending[eng]
            self.pending[eng] = []
        op.xw = self.pending_keys[eng]
        self.pending_keys[eng] = {}
        op.dma = key is not None
        op.key = key
        op.sig = False
        op.signo = 0
        op.cum = 0
        if key is not None:
            self.keycnt[key] = self.keycnt.get(key, 0) + 16
            op.cum = self.keycnt[key]
        op.gidx = len(self.ops)
        self.ops.append(op)
        return op

    def pe(self, fn, reads=(), writes=()):
        return self.add("pe", fn, reads, writes)

    def act(self, fn, reads=(), writes=()):
        return self.add("act", fn, reads, writes)

    def dve(self, fn, reads=(), writes=()):
        return self.add("dve", fn, reads, writes)

    def pool(self, fn, reads=(), writes=()):
        return self.add("pool", fn, reads, writes)

    def dma(self, fn, reads=(), writes=(), key=None, q="sp"):
        assert key is not None
        return self.add(q, fn, reads, writes, key=key)

    def barrier(self, marks):
        k = self.nbar
        self.nbar += 1
        names = []
        for e, fn in marks.items():
            nm = "__bar%d_%s" % (k, e)
            names.append(nm)
            self.add(e, fn, reads=(), writes=[nm])
        snap = {k: v for k, v in self.keycnt.items() if not str(k).startswith("wr")}
        for e in self.ENGS:
            self.pending[e] = self.pending[e] + self._bl(names)
            pk = dict(self.pending_keys[e])
            pk.update(snap)
            self.pending_keys[e] = pk

    def finalize(self):
        ops = self.ops
        for op in ops:
            deps = set()
            for b in op.reads:
                if b.lw is not None:
                    deps.add(b.lw)
            for b in op.writes:
                if b.lw is not None:
                    deps.add(b.lw)
                deps.update(b.rd)
            deps.discard(op)
            op.deps = deps
            for b in op.reads:
                b.rd.append(op)
            for b in op.writes:
                b.lw = op
                b.rd = []
        cnt = {e: 0 for e in self.ENGS}
        for op in ops:
            op.eidx = cnt[op.eng]
            cnt[op.eng] += 1
        seen = {e: {f: -1 for f in self.ENGS} for e in self.ENGS}
        seen_dma = {e: {} for e in self.ENGS}
        for op in ops:
            need_eng = {}
            need_dma = {}
            for k, c in op.xw.items():
                if seen_dma[op.eng].get(k, 0) < c and not (op.dma and op.key == k and op.cum <= c):
                    need_dma[k] = c
            for d in op.deps:
                if d.dma:
                    if seen_dma[op.eng].get(d.key, 0) >= d.cum:
                        continue
                    if need_dma.get(d.key, 0) < d.cum:
                        need_dma[d.key] = d.cum
                else:
                    if d.eng == op.eng:
                        if not SAME_ENGINE_SYNC or d.eng == "pe":
                            continue
                    if seen[op.eng][d.eng] >= d.eidx:
                        continue
                    cur = need_eng.get(d.eng)
                    if cur is None or cur.eidx < d.eidx:
                        need_eng[d.eng] = d
            w = []
            for e, d in need_eng.items():
                d.sig = True
                seen[op.eng][e] = d.eidx
                w.append(("eng", d))
            for k, c in need_dma.items():
                seen_dma[op.eng][k] = c
                w.append(("dma", k, c))
            op.waits = w
        sc = {e: 0 for e in self.ENGS}
        for op in ops:
            if op.sig and not op.dma:
                sc[op.eng] += 1
                op.signo = sc[op.eng]
        self.sigcounts = sc
        return sc

    def emit(self, ename, e, sems, keysems):
        for op in self.ops:
            if op.eng != ename:
                continue
            for w in op.waits:
                if w[0] == "eng":
                    d = w[1]
                    e.wait_ge(sems[d.eng], d.signo)
                else:
                    e.wait_ge(keysems[w[1]], w[2])
            ins = op.fn(e)
            if op.dma:
                ins.then_inc(keysems[op.key], 16)
            elif op.sig:
                ins.then_inc(sems[op.eng], 1)


def run_build(nc, P, final_keys=()):
    P.finalize()
    with ExitStack() as st:
        sems = {e: st.enter_context(nc.semaphore("s_" + e)) for e in P.ENGS}
        keysems = {k: st.enter_context(nc.semaphore("k_" + str(k))) for k in P.keycnt}
        block = st.enter_context(nc.Block())

        @block.sync
        def _(eng):
            P.emit("sp", eng, sems, keysems)
            for k in final_keys:
                eng.wait_ge(keysems[k], P.keycnt[k])

        @block.tensor
        def _(eng):
            P.emit("pe", eng, sems, keysems)

        @block.scalar
        def _(eng):
            P.emit("act", eng, sems, keysems)

        @block.vector
        def _(eng):
            P.emit("dve", eng, sems, keysems)

        @block.gpsimd
        def _(eng):
            P.emit("pool", eng, sems, keysems)


def bc(ap, pos, n):
    dims = [list(d) for d in ap.ap]
    dims.insert(1 + pos, [0, n])
    return bass.AP(ap.tensor, ap.offset, dims)


def segs_of(T):
    out = []
    t = 0
    while t < T:
        n = min(512, T - t)
        out.append((t, n))
        t += n
    return out


def build(NOWN, NPRE, STS, DEC, WSB=36 * 1024):
    NCH = NOWN + 1
    TOK = NCH * 128
    TMAX = max(c1 - c0 for c0, c1 in STS) * 128
    nc = bass.Bass("TRN2", target_bir_lowering=False)

    def din(name, shape):
        return nc.dram_tensor(name, shape, F32, kind="ExternalInput").ap()

    xo = din("xo", [TOK, D])
    xp = din("xp", [max(NPRE, 1) * 128, D])
    nw0 = din("nw0", [1, D])
    nw1 = din("nw1", [1, D])
    w_in0 = din("w_in0", [D, 8192])
    w_out0 = din("w_out0", [D, D])
    w_in1 = din("w_in1", [D, 4608])
    w_out1 = din("w_out1", [D, D])
    convw = din("convw", [3, 1024])
    qnw = din("qnw", [1, 64])
    knw = din("knw", [1, 64])
    sinks = din("sinks", [1, 32])
    cs0T_d = din("cs0T", [2, 128, TOK])
    cs0p_d = din("cs0p", [max(NPRE, 1) * 128, 256])
    wpre_d = din("wpre", [128, max(NPRE, 1) * 4])
    cs1_d = din("cs1", [128, NCH * 64])
    maskT_d = din("maskT", [128, 512])
    kq_d = din("kqdec", [128, 8])
    swam_d = din("swam", [128, 384])
    ident_d = din("ident", [128, 128])
    out = nc.dram_tensor("out", [NOWN * 128, D], F32, kind="ExternalOutput").ap()
    x1s = nc.dram_tensor("x1s", [TOK, D], F32, kind="ExternalOutput").ap() if DEBUG else nc.dram_tensor("x1s", [TOK, D], F32).ap()

    w_in0v = w_in0.rearrange("(kc p) n -> p kc n", p=128)
    w_out0v = w_out0.rearrange("(kc p) n -> p kc n", p=128)
    w_in1v = w_in1.rearrange("(kc p) n -> p kc n", p=128)
    w_out1v = w_out1.rearrange("(kc p) n -> p kc n", p=128)

    P = Prog()
    with ExitStack() as st:
        def sb(name, shape, dt):
            return st.enter_context(nc.sbuf_tensor("sb_" + name, shape, dt))

        psf = [st.enter_context(nc.psum_tensor("ps%d" % i, [128, 512], F32)) for i in range(8)]
        psb = [p[:, :].bitcast(BF16) for p in psf]
        PB = ["psb%d" % i for i in range(8)]

        class Pool_:
            def __init__(s, banks):
                s.banks = banks
                s.i = 0

            def next(s):
                b = s.banks[s.i % len(s.banks)]
                s.i += 1
                return b

        identf = sb("identf", [128, 128], F32)
        identb = sb("identb", [128, 128], BF16)
        nw0T = sb("nw0T", [128, 16], F32)
        nw1T = sb("nw1T", [128, 16], F32)
        hT = sb("hT", [128, 16, TMAX], BF16)
        mixT = sb("mixT", [128, 16, TMAX], BF16)
        WSL = 3
        wr = [sb("wr%d" % i, [128, 16, 512], BF16) for i in range(WSL)]
        xts = [sb("xt%d" % i, [128, D], F32) for i in range(2)]
        hbs = [sb("hb%d" % i, [128, D], BF16) for i in range(2)]
        ssv = sb("ssv", [128, 8], F32)
        Sf = sb("Sf", [128, 4, 512], F32)
        maskT = sb("maskT", [128, 4, 128], F32)
        kq = sb("kq", [128, 8], F32)
        cw = sb("cw", [128, 3, 8], F32)
        pcarry = sb("pcarry", [128, 8, 2], F32)
        hTc = sb("hTc", [128, 16, 2], BF16)
        kT1 = sb("kT1", [128, 4, NCH * 128], BF16)
        vaug = sb("vaug", [128, NCH, 4, 65], BF16)
        cs1 = sb("cs1", [128, NCH, 2, 32], F32)
        swam = sb("swam", [128, 3, 128], BF16)
        swamf = sb("swamf", [128, 3, 128], F32)
        wqb = sb("wqb", [128, 64], F32)
        wkb = sb("wkb", [128, 64], F32)
        snk = sb("snk", [128, 32], F32)
        es = sb("es", [128, 32], F32)
        negc = sb("negc", [128, 1], F32)
        tmpc = sb("tmpc", [128, 4], F32)
        barscr = sb("barscr", [128, 4], F32)

        WS_BYTES = WSB
        arena = sb("arena", [128, WS_BYTES // 2], BF16)

        class Ar:
            def __init__(s):
                s.off = 0

            def alloc(s, shape, dt):
                n = int(np.prod(shape))
                nb = n * (4 if dt == F32 else 2)
                nb_al = (nb + 31) // 32 * 32
                assert s.off + nb_al <= WS_BYTES, ("arena overflow", s.off, nb_al)
                v = arena[:, s.off // 2:(s.off + nb) // 2]
                s.off += nb_al
                if dt == F32:
                    v = v.bitcast(F32)
                if len(shape) == 2:
                    v = v.rearrange("p (a b) -> p a b", a=shape[0])
                elif len(shape) == 3:
                    v = v.rearrange("p (a b c) -> p a b c", a=shape[0], b=shape[1])
                return v

        xcnt = [0]

        w_in0g = w_in0.rearrange("(kc p) (g n) -> p kc g n", p=128, g=8)

        def st_specs():
            sp = []
            for hh in range(4):
                sp += [("g", 0, hh * 256), ("g", 2, hh * 256)]
            for cp in range(4):
                sp += [("g", 4, cp * 256), ("g", 6, cp * 256)]
            for cg in range(4):
                sp.append(("o0", cg * 512))
            sp.append(("i1", 2048))
            for cg in range(4):
                sp += [("i1", cg * 512), ("i1", 2560 + cg * 512)]
            for cg in range(4):
                sp.append(("o1", cg * 512))
            return sp

        wsched = []
        for _ in STS:
            wsched += st_specs()
        widx = [0]
        wiss = [0]

        def wissue(i):
            spec = wsched[i]
            slot = i % WSL
            for part in range(2):
                if spec[0] == "g":
                    src = w_in0g[:, :, spec[1] + part, spec[2]:spec[2] + 256]
                else:
                    v = {"o0": w_out0v, "i1": w_in1v, "o1": w_out1v}[spec[0]]
                    src = v[:, :, spec[1] + part * 256:spec[1] + (part + 1) * 256]
                dst = wr[slot][:, :, part * 256:(part + 1) * 256]
                P.dma(lambda e, src=src, dst=dst: e.dma_start(out=dst, in_=src), writes=["wr%d%s" % (slot, "ab"[part])],
                      key="wr%d" % slot, q="pool")

        wdone = [0]

        def wpump():
            while wiss[0] < len(wsched) and wiss[0] <= wdone[0] + WSL - 1:
                wissue(wiss[0])
                wiss[0] += 1

        def wnext(spec):
            i = widx[0]
            widx[0] += 1
            assert wsched[i] == spec, (i, wsched[i], spec)
            wpump()
            assert wiss[0] > i, (wiss[0], i, wdone[0])
            return i % WSL, ["wr%da" % (i % WSL), "wr%db" % (i % WSL)]

        def wrel():
            wdone[0] = widx[0]
            wpump()

        def rms_rstd(ss_ap, n, tag):
            P.dve(lambda e: e.tensor_scalar(out=ss_ap, in0=ss_ap, scalar1=1.0 / n, scalar2=EPS,
                                            op0=ALU.mult, op1=ALU.add), reads=[tag], writes=[tag])
            P.act(lambda e: e.activation(out=ss_ap, in_=ss_ap, func=AF.Ln), reads=[tag], writes=[tag])
            P.act(lambda e: e.activation(out=ss_ap, in_=ss_ap, func=AF.Exp, scale=-0.5), reads=[tag], writes=[tag])

        def norm_transpose(xt, xtn, hb, hbn, nwT, dstT, dstn, tl, tpbanks):
            P.act(lambda e: e.activation(out=hb[:], in_=xt[:], func=AF.Square, accum_out=ssv[:, 0:1]),
                  reads=[xtn], writes=[hbn, "ss0"])
            rms_rstd(ssv[:, 0:1], D, "ss0")
            P.dve(lambda e: e.tensor_scalar(out=hb[:], in0=xt[:], scalar1=ssv[:, 0:1], scalar2=None, op0=ALU.mult),
                  reads=[xtn, "ss0"], writes=[hbn])
            for g in range(2):
                b = tpbanks[g]
                tpv = psb[b][:, 0:1024].rearrange("p (a b) -> p a b", a=8)
                for j in range(8):
                    kc = g * 8 + j
                    P.pe(lambda e, tpv=tpv, j=j, kc=kc: e.transpose(out=tpv[:, j, :], in_=hb[:, kc * 128:(kc + 1) * 128],
                                                                    identity=identb[:]),
                         reads=[hbn, "identb"], writes=[PB[b], PB[b] + "o"])
                P.dve(lambda e, tpv=tpv, g=g: e.tensor_tensor(out=dstT[:, g * 8:(g + 1) * 8, tl:tl + 128], in0=tpv,
                                                              in1=bc(nwT[:, g * 8:(g + 1) * 8], 1, 128), op=ALU.mult),
                      reads=[PB[b], PB[b] + "o", "nw0T", "nw1T"], writes=[dstn])

        def bar():
            P.barrier({
                "act": lambda e: e.activation(out=barscr[:, 0:1], in_=barscr[:, 0:1], func=AF.Copy),
                "dve": lambda e: e.memset(barscr[:, 1:2], 0.0),
                "pool": lambda e: e.memset(barscr[:, 2:3], 0.0),
            })

        dbg_outs = {}

        def dump(name, ap, shape, dt, reads):
            if True:
                return
            t = nc.dram_tensor("dbg_" + name, [128, int(np.prod(shape))], dt, kind="ExternalOutput").ap()
            pat = {1: None, 2: "p (a b) -> p a b", 3: "p (a b c) -> p a b c"}[len(shape)]
            tv = t if pat is None else (t.rearrange(pat, a=shape[0]) if len(shape) == 2 else t.rearrange(pat, a=shape[0], b=shape[1]))
            P.dma(lambda e: e.dma_start(out=tv, in_=ap), reads=reads, writes=["dbg_" + name], key="st_dbg_" + name)

        def cload(dst, src, name, q="sp", slow=False):
            if slow:
                P.dma(lambda e: e.dma_start(out=dst, in_=src, allow_slow_non_contiguous=True), writes=[name], key="c_" + name, q=q)
            else:
                P.dma(lambda e: e.dma_start(out=dst, in_=src), writes=[name], key="c_" + name, q=q)

        P.dve(lambda e: e.memset(barscr[:], 0.0), writes=["barscr"])
        cload(identf[:], ident_d[:, :], "identf")
        P.dve(lambda e: e.tensor_copy(out=identb[:], in_=identf[:]), reads=["identf"], writes=["identb"])
        cload(nw0T[:], nw0.rearrange("o (kc p) -> p (o kc)", p=128), "nw0T", slow=True)
        cload(nw1T[:], nw1.rearrange("o (kc p) -> p (o kc)", p=128), "nw1T", slow=True)
        for k_ in range(3):
            cload(cw[:, k_, :], convw[k_:k_ + 1, :].rearrange("o (cc p) -> p (o cc)", p=128), "cw%d" % k_, slow=True)
        cload(maskT[:], maskT_d.rearrange("p (h i) -> p h i", h=4), "maskT")
        cload(kq[:], kq_d[:, :], "kq")
        cload(cs1[:], cs1_d.rearrange("p (c t f) -> p c t f", c=NCH, t=2), "cs1")
        cload(swamf[:], swam_d.rearrange("p (a i) -> p a i", a=3), "swamf")
        P.dve(lambda e: e.tensor_copy(out=swam[:], in_=swamf[:]), reads=["swamf"], writes=["swam"])
        cload(wqb[:], bass.AP(qnw.tensor, 0, [[0, 128], [1, 64]]), "wqb")
        cload(wkb[:], bass.AP(knw.tensor, 0, [[0, 128], [1, 64]]), "wkb")
        cload(snk[:], bass.AP(sinks.tensor, 0, [[0, 128], [1, 32]]), "snk")
        P.dve(lambda e: e.tensor_reduce(out=tmpc[:, 0:1], in_=wqb[:], axis=AX.X, op=ALU.max, apply_absolute_value=True), reads=["wqb"], writes=["tmpc0"])
        P.dve(lambda e: e.tensor_reduce(out=tmpc[:, 1:2], in_=wkb[:], axis=AX.X, op=ALU.max, apply_absolute_value=True), reads=["wkb"], writes=["tmpc1"])
        P.dve(lambda e: e.tensor_tensor(out=tmpc[:, 2:3], in0=tmpc[:, 0:1], in1=tmpc[:, 1:2], op=ALU.mult),
              reads=["tmpc0", "tmpc1"], writes=["tmpc2"])
        P.dve(lambda e: e.tensor_scalar(out=negc[:], in0=tmpc[:, 2:3], scalar1=-8.0, scalar2=None, op0=ALU.mult),
              reads=["tmpc2"], writes=["negc"])
        P.act(lambda e: e.activation(out=es[:], in_=snk[:], func=AF.Exp, bias=negc[:, 0:1], scale=1.0),
              reads=["snk", "negc"], writes=["es"])
        P.dve(lambda e: e.memset(vaug[:], 1.0), writes=["vaug_init"])
        P.dve(lambda e: e.memset(Sf[:], 0.0), writes=["Sf_init"])
        P.dve(lambda e: e.memset(pcarry[:], 0.0), writes=["pcarry"])
        P.dve(lambda e: e.memset(hTc[:], 0.0), writes=["hTc"])

        if NPRE > 0:
            ar = Ar()
            wpre = ar.alloc([NPRE, 4], F32)
            csp = [ar.alloc([2, 128], F32) for _ in range(2)]
            hTp = [ar.alloc([16, 128], BF16) for _ in range(2)]
            kb = [ar.alloc([1024], BF16) for _ in range(2)]
            vb = [ar.alloc([4, 256], BF16) for _ in range(2)]
            tt = [ar.alloc([2, 128], F32) for _ in range(4)]
            cload(wpre, wpre_d.rearrange("p (c h) -> p c h", h=4), "wpre")
            places = []
            for t in (hT, mixT):
                for j in range(TMAX // 256):
                    places.append((t, j * 256))
            for t in wr:
                places += [(t, 0), (t, 256)]
            places = places[:8]
            assert len(places) == 8
            WKN = []
            for i, (t, co) in enumerate(places):
                srcc = 1024 + i * 256
                P.dma(lambda e, t=t, co=co, srcc=srcc: e.dma_start(out=t[:, :, co:co + 256], in_=w_in0v[:, :, srcc:srcc + 256]),
                      writes=["wkv_%d" % i], key="wkv", q="pool")
                WKN.append("wkv_%d" % i)
            ppool = Pool_([4, 5, 6])
            for c in range(NPRE):
                xt = xts[c % 2]
                xtn = "xt%d" % (c % 2)
                hb = hbs[c % 2]
                hbn = "hb%d" % (c % 2)
                P.dma(lambda e, xt=xt, c=c: e.dma_start(out=xt[:], in_=xp[c * 128:(c + 1) * 128, :]),
                      writes=[xtn], key=xtn)
                cs = csp[c % 2]
                csn = "csp%d" % (c % 2)
                P.dma(lambda e, cs=cs, c=c: e.dma_start(out=cs, in_=cs0p_d[c * 128:(c + 1) * 128, :].rearrange("p (t f) -> p t f", t=2)),
                      writes=[csn], key=csn)
                hp = hTp[c % 2]
                hpn = "hTp%d" % (c % 2)
                P.act(lambda e, xt=xt, hb=hb: e.activation(out=hb[:], in_=xt[:], func=AF.Square, accum_out=ssv[:, 0:1]),
                      reads=[xtn], writes=[hbn, "ss0"])
                rms_rstd(ssv[:, 0:1], D, "ss0")
                P.dve(lambda e, xt=xt, hb=hb: e.tensor_scalar(out=hb[:], in0=xt[:], scalar1=ssv[:, 0:1], scalar2=None, op0=ALU.mult),
                      reads=[xtn, "ss0"], writes=[hbn])
                for g in range(2):
                    tpv = psb[7][:, 0:1024].rearrange("p (a b) -> p a b", a=8)
                    for j in range(8):
                        kc = g * 8 + j
                        P.pe(lambda e, tpv=tpv, j=j, kc=kc, hb=hb: e.transpose(out=tpv[:, j, :], in_=hb[:, kc * 128:(kc + 1) * 128],
                                                                               identity=identb[:]),
                             reads=[hbn, "identb"], writes=[PB[7]])
                    P.dve(lambda e, tpv=tpv, g=g, hp=hp: e.tensor_tensor(out=hp[:, g * 8:(g + 1) * 8, :], in0=tpv,
                                                                        in1=bc(nw0T[:, g * 8:(g + 1) * 8], 1, 128), op=ALU.mult),
                          reads=[PB[7], "nw0T"], writes=[hpn])
                if c == NPRE - 1:
                    P.act(lambda e, hp=hp: e.copy(out=hTc[:], in_=hp[:, :, 126:128]), reads=[hpn], writes=["hTc"])
                kbc = kb[c % 2]
                kbn = "kb%d" % (c % 2)
                vbc = vb[c % 2]
                vbn = "vb%d" % (c % 2)
                for grp in range(4):
                    b = ppool.next()
                    for kc in range(16):
                        for sub in range(2):
                            (t_, co_) = places[grp * 2 + sub]
                            P.pe(lambda e, b=b, kc=kc, t_=t_, co_=co_, hp=hp, sub=sub: e.matmul(psf[b][:, sub * 256:(sub + 1) * 256], lhsT=hp[:, kc, :],
                                                                                             rhs=t_[:, kc, co_:co_ + 256], start=(kc == 0), stop=(kc == 15)),
                                 reads=[hpn] + WKN, writes=[PB[b]])
                    if grp < 2:
                        kv = psf[b][:, :].rearrange("p (h t f) -> p h t f", h=2, t=2)
                        x1 = kv[:, :, 0, :]
                        x2 = kv[:, :, 1, :]
                        cosb = bc(cs[:, 0, :], 0, 2)
                        sinb = bc(cs[:, 1, :], 0, 2)
                        ko = kbc[:, grp * 512:(grp + 1) * 512].rearrange("p (h t f) -> p h t f", h=2, t=2)
                        P.dve(lambda e, x1=x1, cosb=cosb: e.tensor_tensor(out=tt[0], in0=x1, in1=cosb, op=ALU.mult), reads=[PB[b], csn], writes=["tt0"])
                        P.dve(lambda e, x2=x2, sinb=sinb: e.tensor_tensor(out=tt[1], in0=x2, in1=sinb, op=ALU.mult), reads=[PB[b], csn], writes=["tt1"])
                        P.pool(lambda e, ko=ko: e.tensor_tensor(out=ko[:, :, 0, :], in0=tt[0], in1=tt[1], op=ALU.subtract), reads=["tt0", "tt1"], writes=[kbn + "a%d" % grp])
                        P.dve(lambda e, x2=x2, cosb=cosb: e.tensor_tensor(out=tt[2], in0=x2, in1=cosb, op=ALU.mult), reads=[PB[b], csn], writes=["tt2"])
                        P.dve(lambda e, x1=x1, sinb=sinb: e.tensor_tensor(out=tt[3], in0=x1, in1=sinb, op=ALU.mult), reads=[PB[b], csn], writes=["tt3"])
                        P.pool(lambda e, ko=ko: e.tensor_tensor(out=ko[:, :, 1, :], in0=tt[2], in1=tt[3], op=ALU.add), reads=["tt2", "tt3"], writes=[kbn + "b%d" % grp])
                    else:
                        for h2 in range(2):
                            h = (grp - 2) * 2 + h2
                            P.act(lambda e, b=b, h2=h2, h=h, vbc=vbc, c=c: e.activation(out=vbc[:, h, :], in_=psf[b][:, h2 * 256:(h2 + 1) * 256],
                                                                                      func=AF.Copy, scale=wpre[:, c, h:h + 1]),
                                  reads=[PB[b], "wpre"], writes=[vbn + "_%d" % h])
                for h in range(4):
                    for dh in range(2):
                        P.pe(lambda e, h=h, dh=dh, kbc=kbc, vbc=vbc, c=c: e.matmul(psf[h][:, dh * 256:(dh + 1) * 256],
                                                                                   lhsT=kbc[:, h * 256 + dh * 128:h * 256 + (dh + 1) * 128],
                                                                                   rhs=vbc[:, h, :], start=(c == 0), stop=(c == NPRE - 1)),
                             reads=[kbn + "a%d" % (h // 2), kbn + "b%d" % (h // 2), vbn + "_%d" % h], writes=[PB[h]])
            for h in range(4):
                P.act(lambda e, h=h: e.copy(out=Sf[:, h, :], in_=psf[h][:, :]), reads=[PB[h], "Sf_init"], writes=["Sf%d" % h])
            dump("Sf", Sf[:, :, :], [4, 512], F32, ["Sf%d" % h for h in range(4)])
            bar()

        xbuf = [0]
        def do_st(c0, c1):
            nchs = c1 - c0
            T = nchs * 128
            segs = segs_of(T)
            for c in range(c0, c1):
                i = xbuf[0] % 2
                xbuf[0] += 1
                xt, xtn, hb, hbn = xts[i], "xt%d" % i, hbs[i], "hb%d" % i
                P.dma(lambda e, xt=xt, c=c: e.dma_start(out=xt[:], in_=xo[c * 128:(c + 1) * 128, :]), writes=[xtn], key=xtn)
                norm_transpose(xt, xtn, hb, hbn, nw0T, hT, "hT_%d" % c, (c - c0) * 128, [4, 5])
            HTN = ["hT_%d" % c for c in range(c0, c1)]
            if c0 == 0:
                dump("hT", hT[:, :, 0:T], [16, T], BF16, HTN)

            ar = Ar()
            cosT = ar.alloc([T], F32)
            sinT = ar.alloc([T], F32)
            qT = ar.alloc([2, T], BF16)
            kT = ar.alloc([2, T], BF16)
            gT = ar.alloc([2, T], BF16)
            ktok = ar.alloc([nchs, 256], BF16)
            vtok = ar.alloc([nchs, 256], BF16)
            t4 = [ar.alloc([512], F32) for _ in range(4)]
            PTb = [ar.alloc([128], BF16) for _ in range(2)]
            crs = [ar.alloc([256], F32) for _ in range(2)]
            osb = [ar.alloc([256], F32) for _ in range(2)]
            onb = [ar.alloc([256], BF16) for _ in range(2)]
            Sb = ar.alloc([2, 256], BF16)
            cload(cosT, cs0T_d[0, :, c0 * 128:c1 * 128], "cosT")
            cload(sinT, cs0T_d[1, :, c0 * 128:c1 * 128], "sinT")
            prj = Pool_([0, 1, 2, 3])
            for hh in range(4):
                sa, na = wnext(("g", 0, hh * 256))
                sbl, nb_ = wnext(("g", 2, hh * 256))
                for (nm, dst, off, slot, names) in (("q", qT, 0, sa, na), ("k", kT, 256, sa, na)):
                    for (t0, n) in segs:
                        bks = []
                        for half in range(2):
                            b = prj.next()
                            bks.append(b)
                            for kc in range(16):
                                P.pe(lambda e, b=b, kc=kc, slot=slot, off=off, half=half, t0=t0, n=n:
                                     e.matmul(psf[b][:, 0:n], lhsT=wr[slot][:, kc, off + half * 128:off + (half + 1) * 128],
                                              rhs=hT[:, kc, t0:t0 + n], start=(kc == 0), stop=(kc == 15)),
                                     reads=HTN + names, writes=[PB[b]])
                        b1, b2 = bks
                        dn = "%sT_%d" % (nm, t0)
                        P.dve(lambda e, b1=b1, t0=t0, n=n: e.tensor_tensor(out=t4[0][:, 0:n], in0=psf[b1][:, 0:n], in1=cosT[:, t0:t0 + n], op=ALU.mult),
                              reads=[PB[b1], "cosT"], writes=["t40"])
                        P.dve(lambda e, b2=b2, t0=t0, n=n: e.tensor_tensor(out=t4[1][:, 0:n], in0=psf[b2][:, 0:n], in1=sinT[:, t0:t0 + n], op=ALU.mult),
                              reads=[PB[b2], "sinT"], writes=["t41"])
                        P.pool(lambda e, dst=dst, t0=t0, n=n: e.tensor_tensor(out=dst[:, 0, t0:t0 + n], in0=t4[0][:, 0:n], in1=t4[1][:, 0:n], op=ALU.subtract),
                               reads=["t40", "t41"], writes=[dn + "a"])
                        P.dve(lambda e, b2=b2, t0=t0, n=n: e.tensor_tensor(out=t4[2][:, 0:n], in0=psf[b2][:, 0:n], in1=cosT[:, t0:t0 + n], op=ALU.mult),
                              reads=[PB[b2], "cosT"], writes=["t42"])
                        P.dve(lambda e, b1=b1, t0=t0, n=n: e.tensor_tensor(out=t4[3][:, 0:n], in0=psf[b1][:, 0:n], in1=sinT[:, t0:t0 + n], op=ALU.mult),
                              reads=[PB[b1], "sinT"], writes=["t43"])
                        P.pool(lambda e, dst=dst, t0=t0, n=n: e.tensor_tensor(out=dst[:, 1, t0:t0 + n], in0=t4[2][:, 0:n], in1=t4[3][:, 0:n], op=ALU.add),
                               reads=["t42", "t43"], writes=[dn + "b"])
                QN = ["qT_%d%s" % (t0, s) for (t0, n) in segs for s in "ab"]
                KN = ["kT_%d%s" % (t0, s) for (t0, n) in segs for s in "ab"]
                for (t0, n) in segs:
                    for half in range(2):
                        b = prj.next()
                        for kc in range(16):
                            P.pe(lambda e, b=b, kc=kc, half=half, t0=t0, n=n, sbl=sbl:
                                 e.matmul(psf[b][:, 0:n], lhsT=wr[sbl][:, kc, 256 + half * 128:256 + (half + 1) * 128],
                                          rhs=hT[:, kc, t0:t0 + n], start=(kc == 0), stop=(kc == 15)),
                                 reads=HTN + nb_, writes=[PB[b]])
                        P.act(lambda e, b=b, half=half, t0=t0, n=n: e.activation(out=gT[:, half, t0:t0 + n], in_=psf[b][:, 0:n], func=AF.Silu),
                              reads=[PB[b]], writes=["gT_%d_%d" % (t0, half)])
                GN = ["gT_%d_%d" % (t0, half) for (t0, n) in segs for half in range(2)]
                for cl in range(nchs):
                    tl = cl * 128
                    b = prj.next()
                    for kc in range(16):
                        P.pe(lambda e, b=b, kc=kc, tl=tl, sbl=sbl: e.matmul(psf[b][:, 0:256], lhsT=hT[:, kc, tl:tl + 128], rhs=wr[sbl][:, kc, 0:256],
                                                                            start=(kc == 0), stop=(kc == 15)),
                             reads=HTN + nb_, writes=[PB[b]])
                    P.act(lambda e, b=b, cl=cl: e.copy(out=vtok[:, cl, :], in_=psf[b][:, 0:256]), reads=[PB[b]], writes=["vtok_%d" % cl])
                    tpv = psb[4][:, 0:256].rearrange("p (a b) -> p a b", a=2)
                    for dh in range(2):
                        P.pe(lambda e, tpv=tpv, dh=dh, tl=tl: e.transpose(out=tpv[:, dh, :], in_=kT[:, dh, tl:tl + 128], identity=identb[:]),
                             reads=KN + ["identb"], writes=[PB[4]])
                    P.act(lambda e, tpv=tpv, cl=cl, hh=hh: e.activation(out=ktok[:, cl, :].rearrange("p (a b) -> p a b", a=2), in_=tpv,
                                                                        func=AF.Copy, scale=kq[:, hh:hh + 1]),
                          reads=[PB[4], "kq"], writes=["ktok_%d" % cl])
                if c0 == 0 and hh == 0:
                    dump("qT", qT, [2, T], BF16, QN)
                    dump("kT", kT, [2, T], BF16, KN)
                    dump("gT", gT, [2, T], BF16, GN)
                    dump("ktok", ktok, [nchs, 256], BF16, ["ktok_%d" % i for i in range(nchs)])
                    dump("vtok", vtok, [nchs, 256], BF16, ["vtok_%d" % i for i in range(nchs)])
                P.act(lambda e, hh=hh: e.copy(out=Sb.rearrange("p a b -> p (a b)"), in_=Sf[:, hh, :]), reads=["Sf%d" % hh, "Sf_init"], writes=["Sb"])
                dec = float(DEC[hh])
                for cl in range(nchs):
                    tl = cl * 128
                    i2 = cl % 2
                    PT, cr, o_, on_ = PTb[i2], crs[i2], osb[i2], onb[i2]
                    for dh in range(2):
                        P.pe(lambda e, dh=dh, tl=tl: e.matmul(psf[5][:, 0:128], lhsT=kT[:, dh, tl:tl + 128], rhs=qT[:, dh, tl:tl + 128],
                                                              start=(dh == 0), stop=(dh == 1)),
                             reads=KN + QN, writes=[PB[5]])
                    P.dve(lambda e, PT=PT, hh=hh: e.tensor_tensor(out=PT, in0=psf[5][:, 0:128], in1=maskT[:, hh, :], op=ALU.mult),
                          reads=[PB[5], "maskT"], writes=["PT%d" % i2])
                    for dh in range(2):
                        P.pe(lambda e, dh=dh, cl=cl: e.matmul(psf[7][:, dh * 256:(dh + 1) * 256], lhsT=ktok[:, cl, dh * 128:(dh + 1) * 128],
                                                              rhs=vtok[:, cl, :], start=True, stop=True),
                             reads=["ktok_%d" % cl, "vtok_%d" % cl], writes=[PB[7]])
                    P.pe(lambda e, PT=PT, cl=cl: e.matmul(psf[6][:, 0:256], lhsT=PT, rhs=vtok[:, cl, :], start=True, stop=True),
                         reads=["PT%d" % i2, "vtok_%d" % cl], writes=[PB[6] + "a"])
                    for dh in range(2):
                        P.pe(lambda e, dh=dh, tl=tl: e.matmul(psf[6][:, 256:512], lhsT=qT[:, dh, tl:tl + 128], rhs=Sb[:, dh, :],
                                                              start=(dh == 0), stop=(dh == 1)),
                             reads=QN + ["Sb"], writes=[PB[6] + "b"])
                    P.act(lambda e, cr=cr, hh=hh: e.activation(out=cr, in_=psf[6][:, 256:512], func=AF.Copy, scale=kq[:, 4 + hh:5 + hh]),
                          reads=[PB[6] + "b", "kq"], writes=["cr%d" % i2])
                    P.dve(lambda e, cr=cr, o_=o_: e.tensor_tensor(out=o_, in0=psf[6][:, 0:256], in1=cr, op=ALU.add),
                          reads=[PB[6] + "a", "cr%d" % i2], writes=["o%d" % i2])
                    P.act(lambda e, o_=o_, on_=on_: e.activation(out=on_, in_=o_, func=AF.Square, accum_out=ssv[:, 1:2]),
                          reads=["o%d" % i2], writes=["on%d" % i2, "ss1"])
                    rms_rstd(ssv[:, 1:2], 256, "ss1")
                    P.dve(lambda e, o_=o_, on_=on_: e.tensor_scalar(out=on_, in0=o_, scalar1=ssv[:, 1:2], scalar2=None, op0=ALU.mult),
                          reads=["o%d" % i2, "ss1"], writes=["on%d" % i2])
                    tpv = psb[4][:, 512:768].rearrange("p (a b) -> p a b", a=2)
                    for eh in range(2):
                        P.pe(lambda e, tpv=tpv, eh=eh, on_=on_: e.transpose(out=tpv[:, eh, :], in_=on_[:, eh * 128:(eh + 1) * 128], identity=identb[:]),
                             reads=["on%d" % i2, "identb"], writes=[PB[4] + "o"])
                    P.dve(lambda e, tpv=tpv, hh=hh, tl=tl: e.tensor_tensor(out=mixT[:, hh * 2:hh * 2 + 2, tl:tl + 128], in0=tpv,
                                                                          in1=gT[:, :, tl:tl + 128], op=ALU.mult),
                          reads=[PB[4] + "o"] + GN, writes=["mixT_%d_%d" % (hh, c0 + cl)])
                    P.dve(lambda e, hh=hh, dec=dec: e.scalar_tensor_tensor(out=Sf[:, hh, :], in0=Sf[:, hh, :], scalar=dec, in1=psf[7][:, :],
                                                                          op0=ALU.mult, op1=ALU.add),
                          reads=[PB[7], "Sf%d" % hh, "Sf_init"], writes=["Sf%d" % hh])
                    P.act(lambda e, hh=hh: e.copy(out=Sb.rearrange("p a b -> p (a b)"), in_=Sf[:, hh, :]), reads=["Sf%d" % hh], writes=["Sb"])
                wrel()
            bar()

            ar = Ar()
            pbuf = ar.alloc([2 + T], F32)
            usb = [ar.alloc([512], F32) for _ in range(2)]
            acc = [ar.alloc([512], F32) for _ in range(2)]
            sgc = [ar.alloc([512], F32) for _ in range(2)]
            cnt2 = [0]
            for cp in range(4):
                sa, na = wnext(("g", 4, cp * 256))
                sbl, nb_ = wnext(("g", 6, cp * 256))
                for half in range(2):
                    cc = cp * 2 + half
                    pbn = "pbuf"
                    if c0 == 0:
                        for kc in range(16):
                            P.pe(lambda e, kc=kc, half=half, sa=sa: e.matmul(psf[0][:, 0:2], lhsT=wr[sa][:, kc, 256 + half * 128:256 + (half + 1) * 128],
                                                                             rhs=hTc[:, kc, :], start=(kc == 0), stop=(kc == 15)),
                                 reads=["hTc"] + na, writes=[PB[0]])
                        for kc in range(16):
                            P.pe(lambda e, kc=kc, half=half, sbl=sbl: e.matmul(psf[1][:, 0:2], lhsT=wr[sbl][:, kc, half * 128:(half + 1) * 128],
                                                                               rhs=hTc[:, kc, :], start=(kc == 0), stop=(kc == 15)),
                                 reads=["hTc"] + nb_, writes=[PB[1]])
                        P.act(lambda e: e.copy(out=usb[0][:, 0:2], in_=psf[1][:, 0:2]), reads=[PB[1]], writes=["usb0"])
                        P.dve(lambda e: e.tensor_tensor(out=pbuf[:, 0:2], in0=psf[0][:, 0:2], in1=usb[0][:, 0:2], op=ALU.mult),
                              reads=[PB[0], "usb0"], writes=[pbn])
                    else:
                        P.dve(lambda e, cc=cc: e.tensor_copy(out=pbuf[:, 0:2], in_=pcarry[:, cc, :]), reads=["pcarry"], writes=[pbn])
                    for (t0, n) in segs:
                        i2 = cnt2[0] % 2
                        cnt2[0] += 1
                        bset = [0, 1, 2, 3] if i2 == 0 else [4, 5, 6, 7]
                        specs = [(sa, 0), (sa, 256), (sbl, 0), (sbl, 256)]
                        for (b, (slot, off)) in zip(bset, specs):
                            names = na if slot == sa else nb_
                            for kc in range(16):
                                P.pe(lambda e, b=b, kc=kc, slot=slot, off=off, half=half, t0=t0, n=n:
                                     e.matmul(psf[b][:, 0:n], lhsT=wr[slot][:, kc, off + half * 128:off + (half + 1) * 128],
                                              rhs=hT[:, kc, t0:t0 + n], start=(kc == 0), stop=(kc == 15)),
                                     reads=HTN + names, writes=[PB[b]])
                        bB, bC, bu, bG = bset
                        u_, a_, s_ = usb[i2], acc[i2], sgc[i2]
                        P.act(lambda e, u_=u_, bu=bu, n=n: e.copy(out=u_[:, 0:n], in_=psf[bu][:, 0:n]), reads=[PB[bu]], writes=["usb%d" % i2])
                        P.dve(lambda e, u_=u_, bC=bC, t0=t0, n=n: e.tensor_tensor(out=pbuf[:, 2 + t0:2 + t0 + n], in0=psf[bC][:, 0:n], in1=u_[:, 0:n], op=ALU.mult),
                              reads=[PB[bC], "usb%d" % i2], writes=[pbn])
                        P.dve(lambda e, a_=a_, cc=cc, t0=t0, n=n: e.tensor_scalar(out=a_[:, 0:n], in0=pbuf[:, 2 + t0:2 + t0 + n], scalar1=cw[:, 2, cc:cc + 1],
                                                                                 scalar2=None, op0=ALU.mult),
                              reads=[pbn, "cw0", "cw1", "cw2"], writes=["acc%d" % i2])
                        P.dve(lambda e, a_=a_, cc=cc, t0=t0, n=n: e.scalar_tensor_tensor(out=a_[:, 0:n], in0=pbuf[:, 1 + t0:1 + t0 + n], scalar=cw[:, 1, cc:cc + 1],
                                                                                        in1=a_[:, 0:n], op0=ALU.mult, op1=ALU.add),
                              reads=[pbn, "cw0", "cw1", "cw2", "acc%d" % i2], writes=["acc%d" % i2])
                        P.dve(lambda e, a_=a_, cc=cc, t0=t0, n=n: e.scalar_tensor_tensor(out=a_[:, 0:n], in0=pbuf[:, t0:t0 + n], scalar=cw[:, 0, cc:cc + 1],
                                                                                        in1=a_[:, 0:n], op0=ALU.mult, op1=ALU.add),
                              reads=[pbn, "cw0", "cw1", "cw2", "acc%d" % i2], writes=["acc%d" % i2])
                        P.act(lambda e, s_=s_, bG=bG, n=n: e.activation(out=s_[:, 0:n], in_=psf[bG][:, 0:n], func=AF.Silu), reads=[PB[bG]], writes=["sgc%d" % i2])
                        P.dve(lambda e, a_=a_, bB=bB, n=n: e.tensor_tensor(out=a_[:, 0:n], in0=psf[bB][:, 0:n], in1=a_[:, 0:n], op=ALU.mult),
                              reads=[PB[bB], "acc%d" % i2], writes=["acc%d" % i2])
                        P.pool(lambda e, a_=a_, s_=s_, cc=cc, t0=t0, n=n: e.tensor_tensor(out=mixT[:, 8 + cc, t0:t0 + n], in0=a_[:, 0:n], in1=s_[:, 0:n], op=ALU.mult),
                               reads=["acc%d" % i2, "sgc%d" % i2], writes=["mixTc_%d_%d" % (cc, t0)])
                    P.act(lambda e, cc=cc, T=T: e.copy(out=pcarry[:, cc, :], in_=pbuf[:, T:T + 2]), reads=[pbn], writes=["pcarry"])
                wrel()
            MIXN = ["mixT_%d_%d" % (hh, c) for hh in range(4) for c in range(c0, c1)] + \
                   ["mixTc_%d_%d" % (cc, t0) for cc in range(8) for (t0, n) in segs]
            if c0 == 0:
                dump("mixT", mixT[:, :, 0:T], [16, T], BF16, MIXN)
            bar()

            ar = Ar()
            xpc = [ar.alloc([512], F32) for _ in range(3)]
            ypool = Pool_([0, 1, 2, 3])
            pc = [0]
            for cg in range(4):
                sl, nw_ = wnext(("o0", cg * 512))
                for c in range(c0, c1):
                    tl = (c - c0) * 128
                    b = ypool.next()
                    i3 = pc[0] % 3
                    pc[0] += 1
                    xq = xpc[i3]
                    xqn = "xpc%d" % i3
                    P.dma(lambda e, xq=xq, c=c, cg=cg: e.dma_start(out=xq, in_=xo[c * 128:(c + 1) * 128, cg * 512:(cg + 1) * 512]),
                          writes=[xqn], key=xqn)
                    for kc in range(16):
                        P.pe(lambda e, b=b, kc=kc, tl=tl, sl=sl: e.matmul(psf[b][:, :], lhsT=mixT[:, kc, tl:tl + 128], rhs=wr[sl][:, kc, :],
                                                                          start=(kc == 0), stop=(kc == 15)),
                             reads=MIXN + nw_, writes=[PB[b]])
                    P.dve(lambda e, b=b, xq=xq: e.tensor_tensor(out=xq, in0=psf[b][:, :], in1=xq, op=ALU.add), reads=[PB[b], xqn], writes=[xqn])
                    P.dma(lambda e, xq=xq, c=c, cg=cg: e.dma_start(out=x1s[c * 128:(c + 1) * 128, cg * 512:(cg + 1) * 512], in_=xq),
                          reads=[xqn], writes=["x1s_%d_%d" % (c, cg)], key="st_" + xqn)
                wrel()
            for c in range(c0, c1):
                i = xbuf[0] % 2
                xbuf[0] += 1
                xt, xtn, hb, hbn = xts[i], "xt%d" % i, hbs[i], "hb%d" % i
                P.dma(lambda e, xt=xt, c=c: e.dma_start(out=xt[:], in_=x1s[c * 128:(c + 1) * 128, :]),
                      reads=["x1s_%d_%d" % (c, cg) for cg in range(4)], writes=[xtn], key=xtn)
                norm_transpose(xt, xtn, hb, hbn, nw1T, hT, "hT_%d" % c, (c - c0) * 128, [4, 5])
            bar()

            ar = Ar()
            sq = ar.alloc([512], F32)
            qn = ar.alloc([8, 64], F32)
            rt = [ar.alloc([8, 32], F32) for _ in range(4)]
            tabk = ar.alloc([4, 32], F32)
            tabq = ar.alloc([4, 32], F32)
            kbd = ar.alloc([4, 2, 64], BF16)
            qb = ar.alloc([8, 64], BF16)
            qT1 = ar.alloc([4, 128], BF16)
            sg1 = ar.alloc([512], BF16)
            pT = [[ar.alloc([512], BF16) for _ in range(2)] for _ in range(2)]
            den = ar.alloc([8], F32)
            ao = ar.alloc([4, 64], F32)
            at = ar.alloc([512], BF16)
            xpc = [ar.alloc([512], F32) for _ in range(3)]
            prj = Pool_([0, 1])

            def tables(c, wb, tab, tn):
                for (j, (wlo, t)) in enumerate(((0, 0), (32, 1), (32, 0), (0, 1))):
                    P.dve(lambda e, j=j, wlo=wlo, t=t, c=c: e.tensor_tensor(out=tab[:, j, :], in0=cs1[:, c, t, :], in1=wb[:, wlo:wlo + 32], op=ALU.mult),
                          reads=["cs1", "wqb", "wkb"], writes=[tn])

            def qknorm_rope(b, nh, tab, tn, dst, dstn):
                W = nh * 64
                pv = psf[b][:, 0:W].rearrange("p (h d) -> p h d", h=nh)
                P.act(lambda e: e.activation(out=sq[:, 0:W], in_=psf[b][:, 0:W], func=AF.Square), reads=[PB[b]], writes=["sq"])
                P.dve(lambda e: e.tensor_reduce(out=ssv[:, 0:nh], in_=sq[:, 0:W].rearrange("p (h d) -> p h d", h=nh), axis=AX.X, op=ALU.add),
                      reads=["sq"], writes=["ss0"])
                rms_rstd(ssv[:, 0:nh], 64, "ss0")
                qv = qn[:, 0:nh, :]
                P.dve(lambda e: e.tensor_tensor(out=qv, in0=pv, in1=bc(ssv[:, 0:nh], 1, 64), op=ALU.mult), reads=[PB[b], "ss0"], writes=["qn"])
                x1 = qn[:, 0:nh, 0:32]
                x2 = qn[:, 0:nh, 32:64]
                r = [t[:, 0:nh, :] for t in rt]
                P.pool(lambda e: e.tensor_tensor(out=r[0], in0=x1, in1=bc(tab[:, 0, :], 0, nh), op=ALU.mult), reads=["qn", tn], writes=["rt0"])
                P.pool(lambda e: e.tensor_tensor(out=r[1], in0=x2, in1=bc(tab[:, 1, :], 0, nh), op=ALU.mult), reads=["qn", tn], writes=["rt1"])
                P.dve(lambda e: e.tensor_tensor(out=dst[:, :, 0:32], in0=r[0], in1=r[1], op=ALU.subtract), reads=["rt0", "rt1"], writes=[dstn + "a"])
                P.pool(lambda e: e.tensor_tensor(out=r[2], in0=x2, in1=bc(tab[:, 2, :], 0, nh), op=ALU.mult), reads=["qn", tn], writes=["rt2"])
                P.pool(lambda e: e.tensor_tensor(out=r[3], in0=x1, in1=bc(tab[:, 3, :], 0, nh), op=ALU.mult), reads=["qn", tn], writes=["rt3"])
                P.dve(lambda e: e.tensor_tensor(out=dst[:, :, 32:64], in0=r[2], in1=r[3], op=ALU.add), reads=["rt2", "rt3"], writes=[dstn + "b"])

            sl, nkv = wnext(("i1", 2048))
            for c in range(c0, c1):
                tl = (c - c0) * 128
                b = prj.next()
                for kc in range(16):
                    P.pe(lambda e, b=b, kc=kc, tl=tl, sl=sl: e.matmul(psf[b][:, :], lhsT=hT[:, kc, tl:tl + 128], rhs=wr[sl][:, kc, :],
                                                                      start=(kc == 0), stop=(kc == 15)),
                         reads=["hT_%d" % c] + nkv, writes=[PB[b]])
                P.act(lambda e, b=b, c=c: e.copy(out=vaug[:, c, :, 0:64], in_=psf[b][:, 256:512].rearrange("p (h d) -> p h d", h=4)),
                      reads=[PB[b], "vaug_init"], writes=["vaug_%d" % c])
                tables(c, wkb, tabk, "tabk")
                qknorm_rope(b, 4, tabk, "tabk", kbd[:, :, 0, :], "kbd0")
                P.pool(lambda e: e.tensor_copy(out=kbd[:, :, 1, :], in_=kbd[:, :, 0, :]), reads=["kbd0a", "kbd0b"], writes=["kbd1"])
                tpv = psb[6][:, 0:512].rearrange("p (a b) -> p a b", a=4)
                for hk in range(4):
                    P.pe(lambda e, tpv=tpv, hk=hk: e.transpose(out=tpv[:, hk, :], in_=kbd[:, hk, :, :].rearrange("p a b -> p (a b)"), identity=identb[:]),
                         reads=["kbd0a", "kbd0b", "kbd1", "identb"], writes=[PB[6]])
                P.act(lambda e, tpv=tpv, c=c: e.copy(out=kT1[:, :, c * 128:(c + 1) * 128], in_=tpv), reads=[PB[6]], writes=["kT1_%d" % c])
            wrel()

            spool = Pool_([2, 3, 4])
            for cg in range(4):
                sq_, nq = wnext(("i1", cg * 512))
                sg_, ng = wnext(("i1", 2560 + cg * 512))
                for c in range(max(c0, 1), c1):
                    tl = (c - c0) * 128
                    bq = prj.next()
                    for kc in range(16):
                        P.pe(lambda e, bq=bq, kc=kc, tl=tl, sq_=sq_: e.matmul(psf[bq][:, :], lhsT=hT[:, kc, tl:tl + 128], rhs=wr[sq_][:, kc, :],
                                                                              start=(kc == 0), stop=(kc == 15)),
                             reads=["hT_%d" % c] + nq, writes=[PB[bq]])
                    bg = prj.next()
                    for kc in range(16):
                        P.pe(lambda e, bg=bg, kc=kc, tl=tl, sg_=sg_: e.matmul(psf[bg][:, :], lhsT=hT[:, kc, tl:tl + 128], rhs=wr[sg_][:, kc, :],
                                                                              start=(kc == 0), stop=(kc == 15)),
                             reads=["hT_%d" % c] + ng, writes=[PB[bg]])
                    P.act(lambda e, bg=bg: e.activation(out=sg1, in_=psf[bg][:, :], func=AF.Silu), reads=[PB[bg]], writes=["sg1"])
                    tables(c, wqb, tabq, "tabq")
                    qknorm_rope(bq, 8, tabq, "tabq", qb, "qb")
                    tpv = psb[6][:, 512:1024].rearrange("p (a b) -> p a b", a=4)
                    for pr in range(4):
                        P.pe(lambda e, tpv=tpv, pr=pr: e.transpose(out=tpv[:, pr, :], in_=qb[:, 2 * pr:2 * pr + 2, :].rearrange("p a b -> p (a b)"),
                                                                   identity=identb[:]),
                             reads=["qba", "qbb", "identb"], writes=[PB[6] + "q"])
                    P.act(lambda e, tpv=tpv: e.copy(out=qT1, in_=tpv), reads=[PB[6] + "q"], writes=["qT1"])
                    for half in range(2):
                        for blk in range(2):
                            bs = spool.next()
                            kc_ = c - 1 + blk
                            P.pe(lambda e, bs=bs, half=half, kc_=kc_, cg=cg:
                                 e.matmul(psf[bs][:, :], lhsT=kT1[half * 64:(half + 1) * 64, cg, kc_ * 128:(kc_ + 1) * 128],
                                          rhs=qT1[half * 64:(half + 1) * 64, :, :], start=True, stop=True),
                                 reads=["kT1_%d" % kc_, "qT1"], writes=[PB[bs]])
                            pt = pT[half][blk]
                            ptn = "pT%d%d" % (half, blk)
                            P.act(lambda e, bs=bs, pt=pt: e.activation(out=pt, in_=psf[bs][:, :], func=AF.Exp, bias=negc[:, 0:1], scale=0.125),
                                  reads=[PB[bs], "negc"], writes=[ptn])
                            mi = 2 if blk == 1 else (1 if c == 1 else 0)
                            P.pool(lambda e, pt=pt, mi=mi: e.tensor_tensor(out=pt.rearrange("p (a b) -> p a b", a=4),
                                                                         in0=pt.rearrange("p (a b) -> p a b", a=4),
                                                                         in1=bc(swam[:, mi, :], 0, 4), op=ALU.mult),
                                   reads=[ptn, "swam"], writes=[ptn])
                        pvv = psf[5][:, 0:260].rearrange("p (a b) -> p a b", a=4)
                        for pr in range(4):
                            for blk in range(2):
                                kc_ = c - 1 + blk
                                P.pe(lambda e, pvv=pvv, pr=pr, blk=blk, half=half, kc_=kc_, cg=cg:
                                     e.matmul(pvv[:, pr, :], lhsT=pT[half][blk][:, pr * 128:(pr + 1) * 128], rhs=vaug[:, kc_, cg, :],
                                              start=(blk == 0), stop=(blk == 1)),
                                     reads=["pT%d%d" % (half, blk), "vaug_%d" % kc_, "vaug_init"], writes=[PB[5]])
                        esb = es[:, cg * 8 + half:cg * 8 + half + 1]
                        esv = bass.AP(esb.tensor, esb.offset, [list(esb.ap[0]), [2, 4]])
                        P.dve(lambda e, pvv=pvv, esv=esv, half=half: e.tensor_tensor(out=den[:, half * 4:half * 4 + 4], in0=pvv[:, :, 64], in1=esv, op=ALU.add),
                              reads=[PB[5], "es"], writes=["den"])
                        P.dve(lambda e, half=half: e.reciprocal(out=den[:, half * 4:half * 4 + 4], in_=den[:, half * 4:half * 4 + 4]),
                              reads=["den"], writes=["den"])
                        P.dve(lambda e, pvv=pvv, half=half: e.tensor_tensor(out=ao, in0=pvv[:, :, 0:64], in1=bc(den[:, half * 4:half * 4 + 4], 1, 64), op=ALU.mult),
                              reads=[PB[5], "den"], writes=["ao"])
                        P.pool(lambda e, half=half: e.tensor_tensor(out=at.rearrange("p (a t d) -> p a t d", a=4, t=2)[:, :, half, :], in0=ao,
                                                                    in1=sg1.rearrange("p (a t d) -> p a t d", a=4, t=2)[:, :, half, :], op=ALU.mult),
                               reads=["ao", "sg1"], writes=["at%d" % half])
                    tpv2 = psb[7][:, 0:512].rearrange("p (a b) -> p a b", a=4)
                    for j in range(4):
                        P.pe(lambda e, tpv2=tpv2, j=j: e.transpose(out=tpv2[:, j, :], in_=at[:, j * 128:(j + 1) * 128], identity=identb[:]),
                             reads=["at0", "at1", "identb"], writes=[PB[7]])
                    P.act(lambda e, tpv2=tpv2, cg=cg, tl=tl: e.copy(out=mixT[:, cg * 4:cg * 4 + 4, tl:tl + 128], in_=tpv2),
                          reads=[PB[7]], writes=["attT_%d_%d" % (cg, c)])
                wrel()
            ATN = ["attT_%d_%d" % (cg, c) for cg in range(4) for c in range(max(c0, 1), c1)]

            ypool = Pool_([0, 1, 2, 3])
            for cg in range(4):
                sl, nw_ = wnext(("o1", cg * 512))
                for c in range(max(c0, 1), c1):
                    tl = (c - c0) * 128
                    b = ypool.next()
                    i3 = pc[0] % 3
                    pc[0] += 1
                    xq = xpc[i3]
                    xqn = "xpc%d" % i3
                    P.dma(lambda e, xq=xq, c=c, cg=cg: e.dma_start(out=xq, in_=x1s[c * 128:(c + 1) * 128, cg * 512:(cg + 1) * 512]),
                          reads=["x1s_%d_%d" % (c, cg)], writes=[xqn], key=xqn)
                    for kc in range(16):
                        P.pe(lambda e, b=b, kc=kc, tl=tl, sl=sl: e.matmul(psf[b][:, :], lhsT=mixT[:, kc, tl:tl + 128], rhs=wr[sl][:, kc, :],
                                                                          start=(kc == 0), stop=(kc == 15)),
                             reads=ATN + nw_, writes=[PB[b]])
                    P.dve(lambda e, b=b, xq=xq: e.tensor_tensor(out=xq, in0=psf[b][:, :], in1=xq, op=ALU.add), reads=[PB[b], xqn], writes=[xqn])
                    P.dma(lambda e, xq=xq, c=c, cg=cg: e.dma_start(out=out[(c - 1) * 128:c * 128, cg * 512:(cg + 1) * 512], in_=xq),
                          reads=[xqn], writes=["out_%d_%d" % (c, cg)], key="st_" + xqn)
                wrel()
            bar()

        for (c0_, c1_) in STS:
            do_st(c0_, c1_)

        fk = [k for k in P.keycnt if k.startswith("st_")]
        run_build(nc, P, final_keys=fk)
    return nc, P


def host_tables(NOWN, NPRE, g0, has_pred):
    NCH = NOWN + 1
    TOK = NCH * 128
    f32 = np.float32
    gam = 1.0 - 2.0 ** (-5.0 - np.arange(4, dtype=np.float64))
    inv0 = (1.0 / (f32(THETA) ** (np.arange(0, 256, 2, dtype=f32) / f32(256)))).astype(f32)
    inv1 = (1.0 / (f32(THETA) ** (np.arange(0, 64, 2, dtype=f32) / f32(64)))).astype(f32)
    pos_o = np.maximum(g0 + np.arange(TOK), 0).astype(f32)
    ang = pos_o[:, None] * inv0[None, :]
    cs0T = np.stack([np.cos(ang).T, np.sin(ang).T]).astype(f32)
    npre_t = max(NPRE, 1) * 128
    pos_p = np.maximum(g0 - NPRE * 128 + np.arange(npre_t), 0).astype(f32)
    angp = pos_p[:, None] * inv0[None, :]
    cs0p = np.concatenate([np.cos(angp), np.sin(angp)], axis=1).astype(f32)
    i = np.arange(npre_t, dtype=np.float64)
    wp = (gam[None, :] ** (NPRE * 128 - 1 - i)[:, None]) / 16.0
    wpre = wp.reshape(max(NPRE, 1), 128, 4).transpose(1, 0, 2).reshape(128, -1).astype(f32)
    ang1 = pos_o[:, None] * inv1[None, :]
    cs1 = np.stack([np.cos(ang1), np.sin(ang1)], axis=1).astype(f32)
    cs1 = cs1.reshape(NCH, 128, 2, 32).transpose(1, 0, 2, 3).reshape(128, -1)
    jj = np.arange(128)[:, None]
    ii = np.arange(128)[None, :]
    maskT = np.zeros((128, 4, 128), np.float64)
    for h in range(4):
        maskT[:, h, :] = np.where(ii >= jj, gam[h] ** np.maximum(ii - jj, 0), 0.0) / 16.0
    kq = np.zeros((128, 8), np.float64)
    for h in range(4):
        kq[:, h] = gam[h] ** (127 - np.arange(128)) / 16.0
        kq[:, 4 + h] = gam[h] ** (np.arange(128) + 1)
    mprev = (jj > ii).astype(f32)
    mcur = (jj <= ii).astype(f32)
    mfirst = mprev if has_pred else np.zeros_like(mprev)
    swam = np.stack([mprev, mfirst, mcur], axis=1).reshape(128, 384)
    dec = [float(g ** 128) for g in gam]
    return dict(cs0T=cs0T, cs0p=cs0p, wpre=wpre, cs1=np.ascontiguousarray(cs1, dtype=f32),
                maskT=maskT.reshape(128, 512).astype(f32), kqdec=kq.astype(f32), swam=swam.astype(f32),
                ident=np.eye(128, dtype=f32)), dec


_CACHE = {}


def run_module(x, ev_norm_w, ev_w_in, ev_conv_w, ev_w_out, od_norm_w, od_w_in, od_q_norm_w, od_k_norm_w,
               od_sinks, od_w_out, n_seg=4, STS=None, trace=False):
    x = np.asarray(x, dtype=np.float32)
    B, S, _ = x.shape
    seg = S // n_seg
    NOWN = seg // 128
    NPRE = (n_seg - 1) * NOWN - 1
    if STS is None:
        nch = NOWN + 1
        nst = max(2, -(-nch // 6))
        base, rem = divmod(nch, nst)
        STS = []
        a = 0
        for i in range(nst):
            n = base + (1 if i < rem else 0)
            STS.append((a, a + n))
            a += n
    key = (NOWN, NPRE, tuple(STS))
    gam = 1.0 - 2.0 ** (-5.0 - np.arange(4, dtype=np.float64))
    DEC = [float(g ** 128) for g in gam]
    if key not in _CACHE:
        _CACHE[key] = build(NOWN, NPRE, STS, DEC)
    nc, P = _CACHE[key]
    f = lambda a: np.ascontiguousarray(np.asarray(a, dtype=np.float32))
    common = dict(nw0=f(ev_norm_w).reshape(1, D), nw1=f(od_norm_w).reshape(1, D), w_in0=f(ev_w_in)[0], w_out0=f(ev_w_out)[0],
                  w_in1=f(od_w_in)[0], w_out1=f(od_w_out)[0], convw=f(ev_conv_w)[0], qnw=f(od_q_norm_w).reshape(1, 64),
                  knw=f(od_k_norm_w).reshape(1, 64), sinks=f(od_sinks).reshape(1, 32))
    in_maps = []
    ncores = B * n_seg
    for r in range(ncores):
        b, sg = divmod(r, n_seg)
        start = sg * seg
        g0 = start - 128
        xo = np.zeros(((NOWN + 1) * 128, D), np.float32)
        if g0 >= 0:
            xo[:128] = x[b, g0:start]
        xo[128:] = x[b, start:start + seg]
        xpn = max(NPRE, 1) * 128
        xp = np.zeros((xpn, D), np.float32)
        if g0 > 0:
            xp[xpn - g0:] = x[b, 0:g0]
        tabs, _ = host_tables(NOWN, NPRE, g0, sg > 0)
        m = dict(common)
        m.update(tabs)
        m["xo"] = xo
        m["xp"] = xp
        in_maps.append(m)
    res = run_bass_kernel_spmd(nc, in_maps, core_ids=list(range(ncores)), **({"trace": True} if trace else {}))
    outp = np.zeros((B, S, D), np.float32)
    for r in range(ncores):
        b, sg = divmod(r, n_seg)
        outp[b, sg * seg:(sg + 1) * seg] = res.results[r]["out"]
    return outp, res


def kernel(x, ev_norm_w, ev_w_in, ev_conv_w, ev_w_out, od_norm_w, od_w_in, od_q_norm_w, od_k_norm_w,
           od_sinks, od_w_out):
    outp, _ = run_module(x, ev_norm_w, ev_w_in, ev_conv_w, ev_w_out, od_norm_w, od_w_in, od_q_norm_w,
                         od_k_norm_w, od_sinks, od_w_out)
    return outp
```
